# Optimizing a Trainium2 kernel written in Bass

```python
import math
import jax
import jax.numpy as jnp
from jax import lax
import numpy as np

D_MODEL = 2048
BATCH = 1
SEQ = 8192
DEPTH = 4

GRID_W = 64
CTX_LEN = 256
D_MIX = D_MODEL
LRU_W = D_MIX // 2
LRU_HEADS = 4
LRU_HEAD_DIM = LRU_W // LRU_HEADS
LRU_CONV = 4
LRU_C = 8.0
HY_W = D_MIX - LRU_W
HY_CONV = 3
HY_EMB = 33
HY_BANDS = (HY_EMB - 1) // 2
HY_ORDER_DIM = 64
HY_TARGET = 1e-2
HY_FAST_DECAY_PCT = 0.3
HY_SLOW_DECAY_PCT = 1.5
HY_MAX_DECAY = math.log(HY_TARGET) / HY_FAST_DECAY_PCT
HY_MIN_DECAY = math.log(HY_TARGET) / HY_SLOW_DECAY_PCT
IN_COLS = 2 * LRU_W + 3 * HY_W
D_FF = 5632
N_MOD = 9
EPS = 1e-6

kernel_name = 'hybrid_rglru_hyena_macaron_dit'


def rms_norm(x, g):
    xf = x.astype(jnp.float32)
    y = xf * lax.rsqrt(jnp.mean(xf * xf, axis=-1, keepdims=True) + EPS)
    return (y * g.astype(jnp.float32)).astype(x.dtype)


def modulate(x, g, shift, scale):
    return rms_norm(x, g) * (1 + scale) + shift


def swiglu(u, wg, wu, wd):
    return (jax.nn.silu(u @ wg) * (u @ wu)) @ wd


def ffn_sublayer(h, mod, g, wg, wu, wd):
    shift, scale, gate = mod
    return h + 0.5 * gate * swiglu(modulate(h, g, shift, scale), wg, wu, wd)


def dw_conv(u, w, b):
    K = w.shape[0]
    L = u.shape[1]
    left = (K - 1) // 2
    up = jnp.pad(u, ((0, 0), (left, K - 1 - left), (0, 0)))
    out = b
    for k in range(K):
        out = out + up[:, k:k + L] * w[k]
    return out


def to_col_major(u):
    B, N, C = u.shape
    rows = N // GRID_W
    return u.reshape(B, rows, GRID_W, C).transpose(0, 2, 1, 3).reshape(B, N, C)


def from_col_major(u):
    B, N, C = u.shape
    rows = N // GRID_W
    return u.reshape(B, GRID_W, rows, C).transpose(0, 2, 1, 3).reshape(B, N, C)


def _lin_combine(left, right):
    a1, b1 = left
    a2, b2 = right
    return a1 * a2, a2 * b1 + b2


def rglru_coeffs(xc, wa, ba, wx, bx, lam):
    B, L, _ = xc.shape
    xf = xc.astype(jnp.float32)
    xh = xf.reshape(B, L, LRU_HEADS, LRU_HEAD_DIM)
    r = jax.nn.sigmoid(jnp.einsum('blhd,hde->blhe', xh, wa.astype(jnp.float32)).reshape(B, L, LRU_W) + ba)
    i = jax.nn.sigmoid(jnp.einsum('blhd,hde->blhe', xh, wx.astype(jnp.float32)).reshape(B, L, LRU_W) + bx)
    log_a = -LRU_C * r * jax.nn.softplus(-lam.astype(jnp.float32))
    a = jnp.exp(log_a)
    b = jnp.sqrt(-jnp.expm1(2.0 * log_a)) * (i * xf)
    return a, b


def linear_scan(a, b, h0, reverse):
    A, Bc = lax.associative_scan(_lin_combine, (a, b), axis=1, reverse=reverse)
    if h0 is None:
        return Bc
    return A * h0[:, None, :] + Bc


def bidir_rglru(cv_ctx, cv_lat, wa, ba, wx, bx, lam, need_ctx):
    h_lat = []
    h_ctx = []
    for d in range(2):
        rev = d == 1
        a_c, b_c = rglru_coeffs(cv_ctx, wa[d], ba[d], wx[d], bx[d], lam[d])
        hc = linear_scan(a_c, b_c, None, rev)
        h0 = hc[:, 0] if rev else hc[:, -1]
        a_l, b_l = rglru_coeffs(cv_lat, wa[d], ba[d], wx[d], bx[d], lam[d])
        h_lat.append(linear_scan(a_l, b_l, h0, rev))
        h_ctx.append(hc)
    lat = h_lat[0] + h_lat[1]
    ctx = (h_ctx[0] + h_ctx[1]) if need_ctx else None
    return lat, ctx


def hyena_filters(L, w1, b1, w2, b2, w3, b3, w4, b4, freq):
    f32 = jnp.float32
    t = jnp.linspace(0.0, 1.0, L, dtype=f32)[:, None]
    w = 2.0 * math.pi * jnp.arange(L, dtype=f32) / L
    f = jnp.linspace(1e-4, HY_BANDS - 1, HY_BANDS, dtype=f32)
    ang = w[:, None] * f[None, :]
    z = jnp.concatenate([t, jnp.cos(ang), -jnp.sin(ang)], axis=-1)
    fr = freq.astype(f32)
    h = jnp.sin(fr * (z @ w1.astype(f32) + b1.astype(f32)))
    h = jnp.sin(fr * (h @ w2.astype(f32) + b2.astype(f32)))
    h = jnp.sin(fr * (h @ w3.astype(f32) + b3.astype(f32)))
    k = h @ w4.astype(f32) + b4.astype(f32)
    deltas = jnp.abs(jnp.linspace(HY_MIN_DECAY, HY_MAX_DECAY, HY_W, dtype=f32))
    decay = jnp.exp(-t * deltas[None, :])
    k_fwd = k[:, :HY_W] * decay
    k_bwd = k[:, HY_W:] * decay
    k_full = jnp.concatenate([k_fwd, jnp.zeros((1, HY_W), f32), k_bwd[:0:-1]], axis=0)
    return k_full / jnp.sum(jnp.abs(k_full), axis=0, keepdims=True)


def long_conv(v, k_full, bias):
    L = v.shape[1]
    vf = v.astype(jnp.float32)
    V = jnp.fft.rfft(vf, n=2 * L, axis=1)
    K = jnp.fft.rfft(k_full, n=2 * L, axis=0)
    y = jnp.fft.irfft(V * K[None], n=2 * L, axis=1)[:, :L]
    return (y + vf * bias.astype(jnp.float32)).astype(v.dtype)


def hyena(proj, conv_w, conv_b, k_full, bias):
    z = dw_conv(proj, conv_w, conv_b)
    x0, x1, v = jnp.split(z, 3, axis=-1)
    return long_conv(v * x1, k_full, bias) * x0


def merge_groups(h_lru, yr, y_hy, out_g, w_out):
    lru = h_lru.astype(yr.dtype) * jax.nn.gelu(yr)
    lru = rms_norm(lru, out_g[:LRU_W])
    hy = rms_norm(y_hy, out_g[LRU_W:])
    return jnp.concatenate([lru, hy], axis=-1) @ w_out


def setup_inputs(seed: int = 0) -> dict:
    key = jax.random.key(seed)
    keys = jax.random.split(key, 33)
    f32 = jnp.float32

    def nrm(i, shape, scale):
        return jax.random.normal(keys[i], shape, f32) * scale

    lam_u = jax.random.uniform(keys[19], (DEPTH, 2, LRU_W), f32, 0.9, 0.999)
    lam_s = lam_u ** (1.0 / LRU_C)
    lam = jnp.log(lam_s) - jnp.log1p(-lam_s)
    return {
        'x': nrm(0, (BATCH, SEQ, D_MODEL), 1.0),
        'c': nrm(1, (BATCH, D_MODEL), 1.0),
        'ctx': nrm(2, (BATCH, CTX_LEN, D_MODEL), 1.0),
        'c_ctx': nrm(3, (D_MODEL,), 1.0),
        'ada_w': nrm(4, (DEPTH, D_MODEL, N_MOD * D_MODEL), 0.5 * D_MODEL ** -0.5),
        'ada_b': nrm(5, (DEPTH, N_MOD * D_MODEL), 0.02),
        'norm_g': 1.0 + nrm(6, (DEPTH, 3, D_MODEL), 0.02),
        'ffn_wg': nrm(7, (DEPTH, 2, D_MODEL, D_FF), D_MODEL ** -0.5),
        'ffn_wu': nrm(8, (DEPTH, 2, D_MODEL, D_FF), D_MODEL ** -0.5),
        'ffn_wd': nrm(9, (DEPTH, 2, D_FF, D_MODEL), D_FF ** -0.5),
        'w_in': nrm(10, (DEPTH, D_MODEL, IN_COLS), D_MODEL ** -0.5),
        'w_out': nrm(11, (DEPTH, D_MIX, D_MODEL), D_MIX ** -0.5),
        'out_g': 1.0 + nrm(12, (DEPTH, D_MIX), 0.02),
        'lru_conv_w': nrm(13, (DEPTH, LRU_CONV, LRU_W), LRU_CONV ** -0.5),
        'lru_conv_b': nrm(14, (DEPTH, LRU_W), 0.02),
        'lru_wa': nrm(15, (DEPTH, 2, LRU_HEADS, LRU_HEAD_DIM, LRU_HEAD_DIM), LRU_HEAD_DIM ** -0.5),
        'lru_ba': nrm(16, (DEPTH, 2, LRU_W), 0.02),
        'lru_wx': nrm(17, (DEPTH, 2, LRU_HEADS, LRU_HEAD_DIM, LRU_HEAD_DIM), LRU_HEAD_DIM ** -0.5),
        'lru_bx': nrm(18, (DEPTH, 2, LRU_W), 0.02),
        'lru_lam': lam,
        'hy_conv_w': nrm(20, (DEPTH, HY_CONV, 3 * HY_W), HY_CONV ** -0.5),
        'hy_conv_b': nrm(21, (DEPTH, 3 * HY_W), 0.02),
        'hy_bias': nrm(22, (DEPTH, HY_W), 1.0),
        'filt_w1': nrm(23, (DEPTH, HY_EMB, HY_ORDER_DIM), HY_EMB ** -0.5),
        'filt_b1': nrm(24, (DEPTH, HY_ORDER_DIM), 0.1),
        'filt_w2': nrm(25, (DEPTH, HY_ORDER_DIM, HY_ORDER_DIM), HY_ORDER_DIM ** -0.5),
        'filt_b2': nrm(26, (DEPTH, HY_ORDER_DIM), 0.1),
        'filt_w3': nrm(27, (DEPTH, HY_ORDER_DIM, HY_ORDER_DIM), HY_ORDER_DIM ** -0.5),
        'filt_b3': nrm(28, (DEPTH, HY_ORDER_DIM), 0.1),
        'filt_w4': nrm(29, (DEPTH, HY_ORDER_DIM, 2 * HY_W), HY_ORDER_DIM ** -0.5),
        'filt_b4': nrm(30, (DEPTH, 2 * HY_W), 0.1),
        'filt_freq': 1.0 + nrm(31, (DEPTH, HY_ORDER_DIM), 0.02),
        'final_g': 1.0 + nrm(32, (D_MODEL,), 0.02),
    }


def reference(x, c, ctx, c_ctx, ada_w, ada_b, norm_g, ffn_wg, ffn_wu, ffn_wd, w_in, w_out, out_g,
              lru_conv_w, lru_conv_b, lru_wa, lru_ba, lru_wx, lru_bx, lru_lam,
              hy_conv_w, hy_conv_b, hy_bias, filt_w1, filt_b1, filt_w2, filt_b2, filt_w3, filt_b3,
              filt_w4, filt_b4, filt_freq, final_g):
    n_lat = x.shape[1]
    n_ctx = ctx.shape[1]
    silu_lat = jax.nn.silu(c)[:, None, :]
    silu_ctx = jax.nn.silu(c_ctx)[None, None, :]
    xl = x
    xc = ctx
    for l in range(DEPTH):
        last = l == DEPTH - 1
        col_major = l % 2 == 1
        mod_l = jnp.split(silu_lat @ ada_w[l] + ada_b[l], N_MOD, axis=-1)
        mod_c = jnp.split(silu_ctx @ ada_w[l] + ada_b[l], N_MOD, axis=-1)
        filt = (filt_w1[l], filt_b1[l], filt_w2[l], filt_b2[l], filt_w3[l], filt_b3[l],
                filt_w4[l], filt_b4[l], filt_freq[l])

        xl = ffn_sublayer(xl, mod_l[0:3], norm_g[l, 0], ffn_wg[l, 0], ffn_wu[l, 0], ffn_wd[l, 0])
        xc = ffn_sublayer(xc, mod_c[0:3], norm_g[l, 0], ffn_wg[l, 0], ffn_wu[l, 0], ffn_wd[l, 0])

        ul = modulate(xl, norm_g[l, 1], mod_l[3], mod_l[4])
        uc = modulate(xc, norm_g[l, 1], mod_c[3], mod_c[4])
        if col_major:
            ul = to_col_major(ul)
        pl = ul @ w_in[l]
        pc = uc @ (w_in[l][:, :LRU_W] if last else w_in[l])

        cv_l = dw_conv(pl[..., :LRU_W], lru_conv_w[l], lru_conv_b[l])
        cv_c = dw_conv(pc[..., :LRU_W], lru_conv_w[l], lru_conv_b[l])
        h_lat, h_ctx = bidir_rglru(cv_c, cv_l, lru_wa[l], lru_ba[l], lru_wx[l], lru_bx[l], lru_lam[l], not last)

        k_lat = hyena_filters(n_lat, *filt)
        y_hy_l = hyena(pl[..., 2 * LRU_W:], hy_conv_w[l], hy_conv_b[l], k_lat, hy_bias[l])
        y_lat = merge_groups(h_lat, pl[..., LRU_W:2 * LRU_W], y_hy_l, out_g[l], w_out[l])
        if col_major:
            y_lat = from_col_major(y_lat)
        xl = xl + mod_l[5] * y_lat

        xl = ffn_sublayer(xl, mod_l[6:9], norm_g[l, 2], ffn_wg[l, 1], ffn_wu[l, 1], ffn_wd[l, 1])

        if not last:
            k_ctx = hyena_filters(n_ctx, *filt)
            y_hy_c = hyena(pc[..., 2 * LRU_W:], hy_conv_w[l], hy_conv_b[l], k_ctx, hy_bias[l])
            y_ctx = merge_groups(h_ctx, pc[..., LRU_W:2 * LRU_W], y_hy_c, out_g[l], w_out[l])
            xc = xc + mod_c[5] * y_ctx
            xc = ffn_sublayer(xc, mod_c[6:9], norm_g[l, 2], ffn_wg[l, 1], ffn_wu[l, 1], ffn_wd[l, 1])

    return rms_norm(xl, final_g)
```

```python
import contextlib, math
import numpy as np
import concourse.bass as bass
import concourse.mybir as mybir
from concourse.bass_utils import run_bass_kernel_spmd

F32 = mybir.dt.float32
BF16 = mybir.dt.bfloat16
AF = mybir.ActivationFunctionType
ALU = mybir.AluOpType
AX = mybir.AxisListType


class T:
    __slots__ = ("name", "w", "rc", "rd")

    def __init__(self, name=""):
        self.name = name
        self.w = None
        self.rc = {}
        self.rd = []


class Op:
    __slots__ = ("eng", "fn", "deps", "sig", "kind", "sem", "val", "idx")


NDMA = 8


class Prog:
    ENGS = ["pe", "act", "dve", "pool", "sp"]

    def __init__(self, nc):
        self.nc = nc
        self.ops = {e: [] for e in self.ENGS}
        self.dmas = {e: [] for e in self.ENGS}
        self.csem = {}
        self.dsem = {}

    def setup(self, es):
        nc = self.nc
        for e in ["pe", "act", "dve", "pool"]:
            self.csem[e] = es.enter_context(nc.semaphore("c_" + e))
        for q in ["sp", "pool", "act"]:
            self.dsem[q] = [es.enter_context(nc.semaphore("d_%s%d" % (q, i))) for i in range(NDMA)]

    def _track(self, o, reads, writes):
        deps = []
        for t in reads:
            if t.w is not None:
                deps.append(t.w)
        for t in writes:
            if t.w is not None:
                deps.append(t.w)
            deps.extend(t.rc.values())
            deps.extend(t.rd)
        for t in reads:
            if o.kind == "c":
                t.rc[o.eng] = o
            else:
                t.rd.append(o)
        for t in writes:
            t.w = o
            t.rc = {}
            t.rd = []
        return deps

    def op(self, eng, fn, reads=(), writes=()):
        o = Op()
        o.eng = eng; o.fn = fn; o.kind = "c"; o.sig = False; o.sem = None; o.val = 0
        o.deps = self._track(o, reads, writes)
        for d in o.deps:
            d.sig = True
        o.idx = len(self.ops[eng])
        self.ops[eng].append(o)
        return o

    def dma(self, q, out, in_, reads=(), writes=(), **kw):
        o = Op()
        o.eng = q; o.kind = "d"; o.sig = True
        k = len(self.dmas[q])
        o.sem = self.dsem[q][k % NDMA]
        o.val = 16 * (k // NDMA + 1)
        o.fn = lambda e: e.dma_start(out=out, in_=in_, **kw)
        o.deps = self._track(o, reads, writes)
        if k >= NDMA:
            o.deps.append(self.dmas[q][k - NDMA])
        for d in o.deps:
            d.sig = True
        self.dmas[q].append(o)
        o.idx = len(self.ops[q])
        self.ops[q].append(o)
        return o

    def finish(self):
        nc = self.nc
        for e in self.ENGS:
            cnt = 0
            for o in self.ops[e]:
                if o.kind == "c" and o.sig:
                    cnt += 1
                    o.sem = self.csem[e]
                    o.val = cnt
        lasts = []
        for q in self.dmas:
            for o in self.dmas[q][-NDMA:]:
                lasts.append(o)
        prog = self

        def replay(eng_name, e, final=False):
            waited = {}
            nw = 0
            for o in prog.ops[eng_name]:
                for d in o.deps:
                    if d.kind == "c" and d.eng == eng_name and eng_name == "pe":
                        continue
                    key = id(d.sem)
                    if waited.get(key, 0) >= d.val:
                        continue
                    waited[key] = d.val
                    e.wait_ge(d.sem, d.val)
                    nw += 1
                ins = o.fn(e)
                if o.sig:
                    ins.then_inc(o.sem, 16 if o.kind == "d" else 1)
            if final:
                for d in lasts:
                    key = id(d.sem)
                    if waited.get(key, 0) >= d.val:
                        continue
                    waited[key] = d.val
                    e.wait_ge(d.sem, d.val)
            print("engine %s: %d ops, %d waits" % (eng_name, len(prog.ops[eng_name]), nw))

        with nc.Block() as block:
            @block.tensor
            def _(e):
                replay("pe", e)

            @block.scalar
            def _(e):
                replay("act", e)

            @block.vector
            def _(e):
                replay("dve", e)

            @block.gpsimd
            def _(e):
                replay("pool", e)

            @block.sync
            def _(e):
                replay("sp", e, final=True)


NT = 1056
BLKS = [(0, 512), (512, 1024), (1024, 1056)]
D = 2048
DFF = 5632
EPS = 1e-6


def build_T(t2, t1, final):
    nc = bass.Bass("TRN2", target_bir_lowering=False)
    P = Prog(nc)

    def din(name, shape):
        return nc.dram_tensor(name, list(shape), F32, kind="ExternalInput").ap()

    def dout(name, shape):
        return nc.dram_tensor(name, list(shape), F32, kind="ExternalOutput").ap()

    xT_d = din("xT", [D, NT])
    cvec_d = din("cvec", [128, 16, 2])
    if t2:
        mix_d = din("mix", [D, NT])
        adaw2_d = din("ada_w2", [D, 4 * D])
        adab2_d = din("ada_b2", [128, 64])
        og_d = din("og", [128, 16])
        wout_d = din("w_out", [D, D])
        ng2_d = din("ng2", [128, 16])
        wg2_d = din("wg2", [D, DFF]); wu2_d = din("wu2", [D, DFF]); wd2_d = din("wd2", [DFF, D])
    if t1:
        adaw1_d = din("ada_w1", [D, 5 * D])
        adab1_d = din("ada_b1", [128, 80])
        ng0_d = din("ng0", [128, 16]); ng1_d = din("ng1", [128, 16])
        wg1_d = din("wg1", [D, DFF]); wu1_d = din("wu1", [D, DFF]); wd1_d = din("wd1", [DFF, D])
        win_d = din("w_in", [D, 5120])
        pl_d = dout("pl", [5120, NT])
    if final:
        fg_d = din("fg", [128, 16])
        out_d = dout("out", [D, 1024])
    else:
        xo_d = dout("xo", [D, NT])

    with contextlib.ExitStack() as es:
        def sb(name, shape, dt):
            return es.enter_context(nc.sbuf_tensor(name, list(shape), dt))

        def ps(name):
            return es.enter_context(nc.psum_tensor(name, [128, 512], F32))

        P.setup(es)
        xT = sb("xT_sb", [128, 16, NT], F32); Tx = [T("x%d" % i) for i in range(16)]
        uT = sb("uT_sb", [128, 16, NT], BF16); Tu = [T("u%d" % i) for i in range(16)]
        hT = sb("hT_sb", [128, 4, NT], BF16); Th = [T("h%d" % i) for i in range(4)]
        wbuf = [sb("wbuf%d" % i, [128, 2 * 16 * 256], BF16) for i in range(2)]; Twb = [T("wb%d" % i) for i in range(2)]
        wdbuf = [sb("wdbuf%d" % i, [128, 4, D], BF16) for i in range(2)]; Twd = [T("wd%d" % i) for i in range(2)]
        rstd = sb("rstd", [128, NT], F32); Trs = T("rstd")
        rstd2 = sb("rstd2", [128, NT], F32); Trs2 = T("rstd2")
        stg = [sb("stg%d" % i, [128, NT], F32) for i in range(2)]; Tstg = [T("stg%d" % i) for i in range(2)]
        sqb = [sb("sqb%d" % i, [128, 512], F32) for i in range(2)]; Tsq = [T("sq%d" % i) for i in range(2)]
        sil = [sb("sil%d" % i, [128, 512], BF16) for i in range(2)]; Tsil = [T("sil%d" % i) for i in range(2)]
        ones = sb("ones", [128, 128], F32); Tones = T("ones")
        cv = sb("cv", [128, 16, 2], F32); Tcv = T("cv")
        sbf = sb("sbf", [128, 16, 2], BF16); Tsbf = T("sbf")
        psg = [ps("psg%d" % i) for i in range(2)]; Tpsg = [T("psg%d" % i) for i in range(2)]
        psu = [ps("psu%d" % i) for i in range(2)]; Tpsu = [T("psu%d" % i) for i in range(2)]
        psd = [ps("psd%d" % i) for i in range(2)]; Tpsd = [T("psd%d" % i) for i in range(2)]
        psn = ps("psn"); Tpsn = T("psn")
        psm = ps("psm"); Tpsm = T("psm")
        cnt = {"w": 0, "wd": 0, "stg": 0, "sq": 0, "sil": 0, "g": 0, "u": 0, "d": 0}

        def nxt(k, n=2):
            v = cnt[k] % n
            cnt[k] += 1
            return v

        def small(name, d_ap, shape):
            t = sb("s_" + name, shape, F32); tr = T(name)
            P.dma("sp", t[:], d_ap, writes=[tr])
            return t, tr

        P.op("dve", lambda e: e.memset(ones[:], 1.0), writes=[Tones])
        for kc in range(16):
            P.dma("sp", xT[:, kc, :], xT_d[kc * 128:(kc + 1) * 128, :], writes=[Tx[kc]])
        P.dma("sp", cv[:], cvec_d, writes=[Tcv])
        P.op("act", lambda e: e.activation(sbf[:], cv[:], AF.Silu), reads=[Tcv], writes=[Tsbf])

        def compute_mod(adaw_d, adab_d, nfc, name):
            adab, Tadab = small(name + "_b", adab_d, [128, nfc])
            mod = sb(name, [128, nfc, 2], F32); Tmod = T(name)
            wv = adaw_d.rearrange("(kc p) c -> p kc c", p=128)
            for ti in range(nfc // 4):
                bi = nxt("w")
                bufv = wbuf[bi][:, :].rearrange("p (kc c) -> p kc c", kc=16)
                P.dma("pool", bufv, wv[:, :, ti * 512:(ti + 1) * 512], writes=[Twb[bi]])
                for fcl in range(4):
                    fc = ti * 4 + fcl
                    for kc in range(16):
                        P.op("pe", lambda e, bufv=bufv, fc=fc, fcl=fcl, kc=kc: e.matmul(
                            psm[:, fc * 2:fc * 2 + 2], bufv[:, kc, fcl * 128:(fcl + 1) * 128], sbf[:, kc, :],
                            start=(kc == 0), stop=(kc == 15)), reads=[Twb[bi], Tsbf], writes=[Tpsm])
            pv = psm[:, 0:nfc * 2].rearrange("p (f j) -> p f j", j=2)
            for j in range(2):
                P.op("dve", lambda e, j=j: e.tensor_tensor(mod[:, :, j], pv[:, :, j], adab[:, :], ALU.add),
                     reads=[Tpsm, Tadab], writes=[Tmod])
            return mod, Tmod

        def derive_A(name, mod, Tmod, m_scale, ng, Tng):
            A = sb(name, [128, 16, 2], F32); TA = T(name)
            for j in range(2):
                P.op("dve", lambda e, j=j: e.scalar_tensor_tensor(
                    A[:, :, j], mod[:, m_scale * 16:(m_scale + 1) * 16, j], 1.0, ng[:, :], ALU.add, ALU.mult),
                    reads=[Tmod, Tng], writes=[TA])
            return A, TA

        def derive_hg(name, mod, Tmod, m_gate):
            hg = sb(name, [128, 16, 2], F32); Thg = T(name)
            P.op("dve", lambda e: e.tensor_scalar(hg[:], mod[:, m_gate * 16:(m_gate + 1) * 16, :], 0.5, None, ALU.mult),
                 reads=[Tmod], writes=[Thg])
            return hg, Thg

        def norm_stats(src_fn, kcs, out_rstd, Tout, ndiv, src_reads):
            for b, (c0, c1) in enumerate(BLKS):
                w = c1 - c0
                for i, kc in enumerate(kcs):
                    si = nxt("sq")
                    P.op("act", lambda e, si=si, kc=kc, c0=c0, c1=c1, w=w: e.activation(
                        sqb[si][:, 0:w], src_fn(kc)[:, c0:c1], AF.Square), reads=[src_reads(kc)], writes=[Tsq[si]])
                    P.op("pe", lambda e, si=si, w=w, i=i: e.matmul(
                        psn[:, 0:w], ones[:], sqb[si][:, 0:w], start=(i == 0), stop=(i == len(kcs) - 1)),
                        reads=[Tones, Tsq[si]], writes=[Tpsn])
                P.op("dve", lambda e, c0=c0, c1=c1, w=w: e.tensor_scalar(
                    out_rstd[:, c0:c1], psn[:, 0:w], 1.0 / ndiv, EPS, ALU.mult, ALU.add), reads=[Tpsn], writes=[Tout])
            P.op("act", lambda e: e.activation(out_rstd[:], out_rstd[:], AF.Sqrt), reads=[Tout], writes=[Tout])
            P.op("dve", lambda e: e.reciprocal(out_rstd[:], out_rstd[:]), reads=[Tout], writes=[Tout])

        def norm_modulate(A, TA, mod, Tmod, m_shift):
            norm_stats(lambda kc: xT[:, kc, :], list(range(16)), rstd, Trs, float(D), lambda kc: Tx[kc])
            for kc in range(16):
                si = nxt("stg")
                P.op("pool", lambda e, si=si, kc=kc: e.tensor_tensor(stg[si][:], xT[:, kc, :], rstd[:], ALU.mult),
                     reads=[Tx[kc], Trs], writes=[Tstg[si]])
                for j, (c0, c1) in enumerate([(0, 1024), (1024, NT)]):
                    P.op("act", lambda e, si=si, kc=kc, j=j, c0=c0, c1=c1: e.activation(
                        uT[:, kc, c0:c1], stg[si][:, c0:c1], AF.Identity,
                        bias=mod[:, m_shift * 16 + kc, j:j + 1], scale=A[:, kc, j:j + 1]),
                        reads=[Tstg[si], TA, Tmod], writes=[Tu[kc]])

        def load_wtile(w_d, col0, ncols, half, bi):
            wv = w_d.rearrange("(kc p) f -> p kc f", p=128)
            bufv = wbuf[bi][:, half * 4096:(half + 1) * 4096].rearrange("p (kc c) -> p kc c", kc=16)
            return bufv, wv[:, :, col0:col0 + ncols]

        def ffn(wg_d, wu_d, wd_d, A, TA, mod, Tmod, m_shift, hg, Thg):
            norm_modulate(A, TA, mod, Tmod, m_shift)
            wdv = wd_d.rearrange("(fc p) d -> p fc d", p=128)
            for grp in range(11):
                di = nxt("wd")
                P.dma("pool", wdbuf[di][:], wdv[:, grp * 4:(grp + 1) * 4, :], writes=[Twd[di]])
                for tl in range(2):
                    f0 = (grp * 4 + tl * 2) * 128
                    bi = nxt("w")
                    gv, gsrc = load_wtile(wg_d, f0, 256, 0, bi)
                    uv, usrc = load_wtile(wu_d, f0, 256, 1, bi)
                    P.dma("pool", gv, gsrc, writes=[Twb[bi]])
                    P.dma("pool", uv, usrc, writes=[Twb[bi]])
                    for c in range(2):
                        fcl = tl * 2 + c
                        for b, (c0, c1) in enumerate(BLKS):
                            w = c1 - c0
                            gi = nxt("g"); ui = nxt("u")
                            for kc in range(16):
                                P.op("pe", lambda e, gv=gv, gi=gi, kc=kc, c=c, c0=c0, c1=c1, w=w: e.matmul(
                                    psg[gi][:, 0:w], gv[:, kc, c * 128:(c + 1) * 128], uT[:, kc, c0:c1],
                                    start=(kc == 0), stop=(kc == 15)), reads=[Twb[bi], Tu[kc]], writes=[Tpsg[gi]])
                            for kc in range(16):
                                P.op("pe", lambda e, uv=uv, ui=ui, kc=kc, c=c, c0=c0, c1=c1, w=w: e.matmul(
                                    psu[ui][:, 0:w], uv[:, kc, c * 128:(c + 1) * 128], uT[:, kc, c0:c1],
                                    start=(kc == 0), stop=(kc == 15)), reads=[Twb[bi], Tu[kc]], writes=[Tpsu[ui]])
                            si = nxt("sil")
                            P.op("act", lambda e, si=si, gi=gi, w=w: e.activation(sil[si][:, 0:w], psg[gi][:, 0:w], AF.Silu),
                                 reads=[Tpsg[gi]], writes=[Tsil[si]])
                            P.op("dve", lambda e, si=si, ui=ui, fcl=fcl, c0=c0, c1=c1, w=w: e.tensor_tensor(
                                hT[:, fcl, c0:c1], sil[si][:, 0:w], psu[ui][:, 0:w], ALU.mult),
                                reads=[Tsil[si], Tpsu[ui]], writes=[Th[fcl]])
                for dc in range(16):
                    for b, (c0, c1) in enumerate(BLKS):
                        w = c1 - c0
                        j = 0 if b < 2 else 1
                        pi = nxt("d")
                        for fcl in range(4):
                            P.op("pe", lambda e, pi=pi, di=di, fcl=fcl, dc=dc, c0=c0, c1=c1, w=w: e.matmul(
                                psd[pi][:, 0:w], wdbuf[di][:, fcl, dc * 128:(dc + 1) * 128], hT[:, fcl, c0:c1],
                                start=(fcl == 0), stop=(fcl == 3)), reads=[Twd[di], Th[fcl]], writes=[Tpsd[pi]])
                        P.op("dve", lambda e, pi=pi, dc=dc, j=j, c0=c0, c1=c1, w=w: e.scalar_tensor_tensor(
                            xT[:, dc, c0:c1], psd[pi][:, 0:w], hg[:, dc, j:j + 1], xT[:, dc, c0:c1], ALU.mult, ALU.add),
                            reads=[Tpsd[pi], Thg, Tx[dc]], writes=[Tx[dc]])

        def proj(w_d, nchunks, evac):
            for ti in range(nchunks // 2):
                bi = nxt("w")
                wv_, src = load_wtile(w_d, ti * 256, 256, 0, bi)
                P.dma("pool", wv_, src, writes=[Twb[bi]])
                for c in range(2):
                    fo = ti * 2 + c
                    for b, (c0, c1) in enumerate(BLKS):
                        w = c1 - c0
                        pi = nxt("d")
                        for kc in range(16):
                            P.op("pe", lambda e, wv_=wv_, pi=pi, kc=kc, c=c, c0=c0, c1=c1, w=w: e.matmul(
                                psd[pi][:, 0:w], wv_[:, kc, c * 128:(c + 1) * 128], uT[:, kc, c0:c1],
                                start=(kc == 0), stop=(kc == 15)), reads=[Twb[bi], Tu[kc]], writes=[Tpsd[pi]])
                        evac(fo, b, c0, c1, w, psd[pi], Tpsd[pi])

        if t2:
            mod2, Tmod2 = compute_mod(adaw2_d, adab2_d, 64, "mod2")
        if t1:
            mod1, Tmod1 = compute_mod(adaw1_d, adab1_d, 80, "mod1")

        if t2:
            og, Tog = small("og", og_d, [128, 16])
            ng2, Tng2 = small("ng2", ng2_d, [128, 16])
            halves = [list(range(0, 8)), list(range(8, 16))]
            rs = [(rstd, Trs), (rstd2, Trs2)]
            for hf in range(2):
                out_r, Tout = rs[hf]
                accs = [(psn, Tpsn), (psd[0], Tpsd[0]), (psd[1], Tpsd[1])]
                for i, kc in enumerate(halves[hf]):
                    si = nxt("stg")
                    P.dma("sp", stg[si][:], mix_d[kc * 128:(kc + 1) * 128, :], writes=[Tstg[si]])
                    for b, (c0, c1) in enumerate(BLKS):
                        w = c1 - c0
                        qi = nxt("sq")
                        P.op("act", lambda e, qi=qi, si=si, c0=c0, c1=c1, w=w: e.activation(
                            sqb[qi][:, 0:w], stg[si][:, c0:c1], AF.Square), reads=[Tstg[si]], writes=[Tsq[qi]])
                        acc, Tacc = accs[b]
                        P.op("pe", lambda e, acc=acc, qi=qi, w=w, i=i: e.matmul(
                            acc[:, 0:w], ones[:], sqb[qi][:, 0:w], start=(i == 0), stop=(i == 7)),
                            reads=[Tones, Tsq[qi]], writes=[Tacc])
                for b, (c0, c1) in enumerate(BLKS):
                    w = c1 - c0
                    acc, Tacc = accs[b]
                    P.op("dve", lambda e, acc=acc, c0=c0, c1=c1, w=w, out_r=out_r: e.tensor_scalar(
                        out_r[:, c0:c1], acc[:, 0:w], 1.0 / 1024.0, EPS, ALU.mult, ALU.add), reads=[Tacc], writes=[Tout])
                P.op("act", lambda e, out_r=out_r: e.activation(out_r[:], out_r[:], AF.Sqrt), reads=[Tout], writes=[Tout])
                P.op("dve", lambda e, out_r=out_r: e.reciprocal(out_r[:], out_r[:]), reads=[Tout], writes=[Tout])
            for kc in range(16):
                out_r, Tout = rs[kc // 8]
                si = nxt("stg")
                P.dma("sp", stg[si][:], mix_d[kc * 128:(kc + 1) * 128, :], writes=[Tstg[si]])
                P.op("pool", lambda e, si=si, out_r=out_r: e.tensor_tensor(stg[si][:], stg[si][:], out_r[:], ALU.mult),
                     reads=[Tstg[si], Tout], writes=[Tstg[si]])
                P.op("act", lambda e, si=si, kc=kc: e.activation(
                    uT[:, kc, :], stg[si][:], AF.Identity, scale=og[:, kc:kc + 1]),
                    reads=[Tstg[si], Tog], writes=[Tu[kc]])

            def evac_wout(fo, b, c0, c1, w, pap, Tp):
                j = 0 if b < 2 else 1
                P.op("dve", lambda e: e.scalar_tensor_tensor(
                    xT[:, fo, c0:c1], pap[:, 0:w], mod2[:, fo, j:j + 1], xT[:, fo, c0:c1], ALU.mult, ALU.add),
                    reads=[Tp, Tmod2, Tx[fo]], writes=[Tx[fo]])
            proj(wout_d, 16, evac_wout)
            A2, TA2 = derive_A("A2", mod2, Tmod2, 2, ng2, Tng2)
            hg2, Thg2 = derive_hg("hg2", mod2, Tmod2, 3)
            ffn(wg2_d, wu2_d, wd2_d, A2, TA2, mod2, Tmod2, 1, hg2, Thg2)

        if t1:
            ng0, Tng0 = small("ng0", ng0_d, [128, 16])
            ng1, Tng1 = small("ng1", ng1_d, [128, 16])
            A0, TA0 = derive_A("A0", mod1, Tmod1, 1, ng0, Tng0)
            hg0, Thg0 = derive_hg("hg0", mod1, Tmod1, 2)
            ffn(wg1_d, wu1_d, wd1_d, A0, TA0, mod1, Tmod1, 0, hg0, Thg0)
            A1, TA1 = derive_A("A1", mod1, Tmod1, 4, ng1, Tng1)
            norm_modulate(A1, TA1, mod1, Tmod1, 3)
            state = {"si": None}

            def evac_pl(fo, b, c0, c1, w, pap, Tp):
                if b == 0:
                    state["si"] = nxt("stg")
                si = state["si"]
                eng = "act" if (fo + b) % 2 == 0 else "dve"
                if eng == "act":
                    P.op("act", lambda e: e.activation(stg[si][:, c0:c1], pap[:, 0:w], AF.Copy), reads=[Tp], writes=[Tstg[si]])
                else:
                    P.op("dve", lambda e: e.tensor_copy(stg[si][:, c0:c1], pap[:, 0:w]), reads=[Tp], writes=[Tstg[si]])
                if b == 2:
                    P.dma("sp", pl_d[fo * 128:(fo + 1) * 128, :], stg[si][:], reads=[Tstg[si]])
            proj(win_d, 40, evac_pl)

        if final:
            fg, Tfg = small("fg", fg_d, [128, 16])
            norm_stats(lambda kc: xT[:, kc, :], list(range(16)), rstd, Trs, float(D), lambda kc: Tx[kc])
            for kc in range(16):
                si = nxt("stg")
                P.op("pool", lambda e, si=si, kc=kc: e.tensor_tensor(stg[si][:, 0:1024], xT[:, kc, 0:1024], rstd[:, 0:1024], ALU.mult),
                     reads=[Tx[kc], Trs], writes=[Tstg[si]])
                P.op("act", lambda e, si=si, kc=kc: e.activation(stg[si][:, 0:1024], stg[si][:, 0:1024], AF.Identity, scale=fg[:, kc:kc + 1]),
                     reads=[Tstg[si], Tfg], writes=[Tstg[si]])
                P.dma("sp", out_d[kc * 128:(kc + 1) * 128, :], stg[si][:, 0:1024], reads=[Tstg[si]])
        else:
            for kc in range(16):
                P.dma("sp", xo_d[kc * 128:(kc + 1) * 128, :], xT[:, kc, :], reads=[Tx[kc]])
        P.finish()
    return nc


WP = 8454
NTOK = 8448
L = 8192
LC = 256
GC = 16
MAGIC = 12582912.0
TWO_PI = 2.0 * math.pi


def mconsts(j):
    f64 = np.float64
    n = np.arange(128, dtype=f64)
    c = {}
    ang = 2 * np.pi * np.outer(np.arange(64), np.arange(128)) / 128.0
    c["F1"] = np.concatenate([np.cos(ang), -np.sin(ang)], 1).astype(np.float32)
    angt = 2 * np.pi * np.outer(n, n) / 16384.0
    c["Tw"] = np.stack([np.cos(angt), -np.sin(angt)], 1).astype(np.float32)
    a2 = 2 * np.pi * np.outer(n, n) / 128.0
    c["F2"] = np.stack([np.cos(a2), np.sin(a2)], 1).astype(np.float32)
    c["GG"] = np.concatenate([np.cos(a2), np.sin(a2), -np.sin(a2), np.cos(a2)], 1).astype(np.float32)
    a3 = 2 * np.pi * np.outer(n, np.arange(64)) / 128.0
    c["Cfin"] = (np.stack([np.cos(a3), -np.sin(a3)], 1) / 16384.0).astype(np.float32)
    f32 = np.float32

    def zmat(Lx):
        t = np.linspace(0.0, 1.0, Lx, dtype=f32)[:, None]
        w = (f32(2.0 * math.pi) * np.arange(Lx, dtype=f32) / f32(Lx)).astype(f32)
        f = np.linspace(1e-4, 15, 16, dtype=f32)
        ang = (w[:, None] * f[None, :]).astype(f32)
        z = np.concatenate([t, np.cos(ang), -np.sin(ang)], -1).astype(f32)
        return np.ascontiguousarray(z.T), t
    c["zlat"], tl = zmat(L)
    c["zctx"], tc = zmat(LC)
    max_decay = math.log(1e-2) / 0.3
    min_decay = math.log(1e-2) / 1.5
    deltas = np.abs(np.linspace(min_decay, max_decay, 1024, dtype=f32))[128 * j:128 * (j + 1)]
    c["dec_lat"] = np.ascontiguousarray(np.exp(-tl * deltas[None, :]).astype(f32).T)
    c["dec_ctx"] = np.ascontiguousarray(np.exp(-tc * deltas[None, :]).astype(f32).T)
    return c


def build_M():
    nc = bass.Bass("TRN2", target_bir_lowering=False)
    P = Prog(nc)

    def din(name, shape):
        return nc.dram_tensor(name, list(shape), F32, kind="ExternalInput").ap()

    def dout(name, shape):
        return nc.dram_tensor(name, list(shape), F32, kind="ExternalOutput").ap()

    pin_d = din("pin", [6, 128, WP])
    lcw_d = din("lcw", [128, 2, 4]); lcb_d = din("lcb", [128, 2])
    wa_d = din("wa", [128, 2, 2, 128]); wx_d = din("wx", [128, 2, 2, 128])
    lvec_d = din("lvec", [128, 2, 3])
    own_d = din("own", [128, 2])
    hcw_d = din("hcw", [128, 3, 3]); hcb_d = din("hcb", [128, 3]); hbias_d = din("hbias", [128, 1])
    mw1_d = din("mw1", [33, 64]); mw23_d = din("mw23", [64, 2, 64]); mb_d = din("mb", [64, 4])
    w4_d = din("w4", [65, 2, 128])
    F1_d = din("F1", [64, 256]); Tw_d = din("Tw", [128, 2, 128]); F2_d = din("F2", [128, 2, 128])
    GG_d = din("GG", [128, 512]); Cfin_d = din("Cfin", [128, 2, 64])
    zlat_d = din("zlat", [33, L]); zctx_d = din("zctx", [33, LC])
    declat_d = din("dec_lat", [128, L]); decctx_d = din("dec_ctx", [128, LC])
    lru_d = dout("lru", [128, NTOK]); hy_d = dout("hy", [128, NTOK])
    scr_vx = nc.dram_tensor("scr_vx", [128, NTOK], F32).ap()
    scr_kf = nc.dram_tensor("scr_kf", [128, L], F32).ap()
    scr_kb = nc.dram_tensor("scr_kb", [128, L], F32).ap()
    scr_y = nc.dram_tensor("scr_y", [128, L], F32).ap()
    own_chunk = None

    with contextlib.ExitStack() as es:
        def sb(name, shape, dt=F32):
            return es.enter_context(nc.sbuf_tensor(name, list(shape), dt))

        P.setup(es)
        G = [sb("G%d" % i, [128, WP]) for i in range(4)]; TG = [T("G%d" % i) for i in range(4)]
        S = sb("S", [128, 3, 4096]); TS = [T("S%d" % i) for i in range(3)]
        pss = [es.enter_context(nc.psum_tensor("ps%d" % i, [128, 512], F32)) for i in range(8)]
        Tps = [T("ps%d" % i) for i in range(8)]
        pcnt = [0]

        def nps():
            v = pcnt[0] % 8
            pcnt[0] += 1
            return v

        def small(name, d_ap, shape):
            t = sb("s_" + name, shape); tr = T(name)
            P.dma("sp", t[:], d_ap, writes=[tr])
            return t, tr

        lcw, Tlcw = small("lcw", lcw_d, [128, 2, 4]); lcb, Tlcb = small("lcb", lcb_d, [128, 2])
        wa, Twa = small("wa", wa_d, [128, 2, 2, 128]); wx, Twx = small("wx", wx_d, [128, 2, 2, 128])
        lvec, Tlvec = small("lvec", lvec_d, [128, 2, 3])
        own, Town = small("own", own_d, [128, 2])
        hcw, Thcw = small("hcw", hcw_d, [128, 3, 3]); hcb, Thcb = small("hcb", hcb_d, [128, 3])
        hbias, Thbias = small("hbias", hbias_d, [128, 1])
        mw1, Tmw1 = small("mw1", mw1_d, [33, 64]); mw23, Tmw23 = small("mw23", mw23_d, [64, 2, 64])
        mb, Tmb = small("mb", mb_d, [64, 4]); w4, Tw4 = small("w4", w4_d, [65, 2, 128])
        F1, TF1 = small("F1", F1_d, [64, 256]); Tw, TTw = small("Tw", Tw_d, [128, 2, 128])
        F2, TF2 = small("F2", F2_d, [128, 2, 128]); GG, TGG = small("GG", GG_d, [128, 512])
        Cfin, TCfin = small("Cfin", Cfin_d, [128, 2, 64])
        decctx, Tdecctx = small("decctx", decctx_d, [128, LC])

        sp_t = sb("sp_t", [128, 8, 2]); Tsp = T("sp")
        nsc = sb("nsc", [128, 2]); Tnsc = T("nsc")
        lam = lvec[:, :, 2]
        AXv, Ev, Zv, Z2v, Pv, Mv, DENv = [sp_t[:, i, :] for i in range(7)]
        P.op("dve", lambda e: e.tensor_scalar(AXv, lam, -1.0, None, ALU.mult), reads=[Tlvec], writes=[Tsp])
        P.op("dve", lambda e: e.tensor_tensor(AXv, AXv, lam, ALU.max), reads=[Tlvec, Tsp], writes=[Tsp])
        P.op("act", lambda e: e.activation(Ev, AXv, AF.Exp, scale=-1.0), reads=[Tsp], writes=[Tsp])
        P.op("dve", lambda e: e.tensor_scalar(DENv, Ev, 2.0, None, ALU.add), reads=[Tsp], writes=[Tsp])
        P.op("dve", lambda e: e.reciprocal(DENv, DENv), reads=[Tsp], writes=[Tsp])
        P.op("dve", lambda e: e.tensor_tensor(Zv, Ev, DENv, ALU.mult), reads=[Tsp], writes=[Tsp])
        P.op("dve", lambda e: e.tensor_tensor(Z2v, Zv, Zv, ALU.mult), reads=[Tsp], writes=[Tsp])
        P.op("dve", lambda e: e.memset(Pv, 1.0 / 15.0), reads=[Tsp], writes=[Tsp])
        for kk in [13.0, 11.0, 9.0, 7.0, 5.0, 3.0, 1.0]:
            P.op("dve", lambda e: e.tensor_tensor(Pv, Pv, Z2v, ALU.mult), reads=[Tsp], writes=[Tsp])
            P.op("dve", lambda e, kk=kk: e.tensor_scalar(Pv, Pv, 1.0 / kk, None, ALU.add), reads=[Tsp], writes=[Tsp])
        P.op("dve", lambda e: e.tensor_tensor(Pv, Pv, Zv, ALU.mult), reads=[Tsp], writes=[Tsp])
        P.op("dve", lambda e: e.tensor_scalar(Mv, lam, -1.0, 0.0, ALU.mult, ALU.max), reads=[Tlvec, Tsp], writes=[Tsp])
        P.op("dve", lambda e: e.scalar_tensor_tensor(Pv, Pv, 2.0, Mv, ALU.mult, ALU.add), reads=[Tsp], writes=[Tsp])
        P.op("dve", lambda e: e.tensor_scalar(nsc[:, :], Pv, -8.0, None, ALU.mult), reads=[Tsp], writes=[Tnsc])

        SEGS_PAD = [(0, 256, 1), (256, NTOK, 4)]
        for ch in range(2):
            P.dma("sp", G[ch][:, :], pin_d[ch], writes=[TG[ch]])
            eng = "dve"
            for (t0, t1, off) in SEGS_PAD:
                n = t1 - t0
                src = lambda k, ch=ch, t0=t0, off=off, n=n: G[ch][:, t0 + off - 1 + k: t0 + off - 1 + k + n]
                dst = G[2 + ch][:, t0:t1]
                P.op(eng, lambda e, src=src, dst=dst, ch=ch: e.tensor_scalar(
                    dst, src(0), lcw[:, ch, 0:1], lcb[:, ch:ch + 1], ALU.mult, ALU.add),
                    reads=[TG[ch], Tlcw, Tlcb], writes=[TG[2 + ch]])
                for k in range(1, 4):
                    P.op(eng, lambda e, src=src, dst=dst, ch=ch, k=k: e.scalar_tensor_tensor(
                        dst, src(k), lcw[:, ch, k:k + 1], dst, ALU.mult, ALU.add),
                        reads=[TG[ch], Tlcw], writes=[TG[2 + ch]])
        P.op("dve", lambda e: e.tensor_scalar(G[0][:, 0:NTOK], G[2][:, 0:NTOK], own[:, 0:1], None, ALU.mult),
             reads=[TG[2], Town], writes=[TG[0]])
        P.op("dve", lambda e: e.scalar_tensor_tensor(G[0][:, 0:NTOK], G[3][:, 0:NTOK], own[:, 1:2], G[0][:, 0:NTOK], ALU.mult, ALU.add),
             reads=[TG[3], Town, TG[0]], writes=[TG[0]])
        cvown = G[0]; Tcvown = TG[0]
        hsum = G[1]; Thsum = TG[1]
        state = sb("state", [128, 1]); Tstate = T("state")
        SEGS = [(0, 256), (256, 4352), (4352, NTOK)]
        Rb, Ib, Tb = S[:, 0, :], S[:, 1, :], S[:, 2, :]
        TR, TI, TT = TS

        def rev(ap2d, n):
            return bass.AP(ap2d.tensor, ap2d.offset + (n - 1), [list(ap2d.ap[0]), [-1, n]])

        for d in range(2):
            order = SEGS if d == 0 else [SEGS[0], SEGS[2], SEGS[1]]
            for si_, (t0, t1) in enumerate(order):
                n = t1 - t0
                nb = (n + 511) // 512
                for b in range(nb):
                    c0 = b * 512; w = min(512, n - c0)
                    for gi, (wmat, Twm, dstb, Tdst, bcol) in enumerate([(wa, Twa, Rb, TR, 0), (wx, Twx, Ib, TI, 1)]):
                        pi = nps()
                        for kc in range(2):
                            P.op("pe", lambda e, pi=pi, wmat=wmat, kc=kc, t0=t0, c0=c0, w=w, d=d: e.matmul(
                                pss[pi][:, 0:w], wmat[:, d, kc, :], G[2 + kc][:, t0 + c0:t0 + c0 + w],
                                start=(kc == 0), stop=(kc == 1)), reads=[Twm, TG[2 + kc]], writes=[Tps[pi]])
                        P.op("act", lambda e, pi=pi, dstb=dstb, c0=c0, w=w, d=d, bcol=bcol: e.activation(
                            dstb[:, c0:c0 + w], pss[pi][:, 0:w], AF.Sigmoid, bias=lvec[:, d, bcol:bcol + 1]),
                            reads=[Tps[pi], Tlvec], writes=[Tdst])
                P.op("act", lambda e, n=n, d=d: e.activation(Rb[:, 0:n], Rb[:, 0:n], AF.Exp, scale=nsc[:, d:d + 1]),
                     reads=[TR, Tnsc], writes=[TR])
                P.op("dve", lambda e, n=n: e.tensor_tensor(Tb[:, 0:n], Rb[:, 0:n], Rb[:, 0:n], ALU.mult), reads=[TR], writes=[TT])
                P.op("dve", lambda e, n=n: e.tensor_scalar(Tb[:, 0:n], Tb[:, 0:n], -1.0, 1.0, ALU.mult, ALU.add), reads=[TT], writes=[TT])
                P.op("act", lambda e, n=n: e.activation(Tb[:, 0:n], Tb[:, 0:n], AF.Sqrt), reads=[TT], writes=[TT])
                P.op("pool", lambda e, n=n: e.tensor_tensor(Ib[:, 0:n], Ib[:, 0:n], Tb[:, 0:n], ALU.mult), reads=[TI, TT], writes=[TI])
                P.op("pool", lambda e, n=n, t0=t0, t1=t1: e.tensor_tensor(Ib[:, 0:n], Ib[:, 0:n], cvown[:, t0:t1], ALU.mult),
                     reads=[TI, Tcvown], writes=[TI])
                init = 0.0 if si_ == 0 else state[:, 0:1]
                rds = [TR, TI] + ([] if si_ == 0 else [Tstate])
                if d == 0:
                    P.op("dve", lambda e, n=n, init=init: e.tensor_tensor_scan(Tb[:, 0:n], Rb[:, 0:n], Ib[:, 0:n], init, ALU.mult, ALU.add),
                         reads=rds, writes=[TT])
                    P.op("dve", lambda e, n=n: e.tensor_copy(state[:, 0:1], Tb[:, n - 1:n]), reads=[TT], writes=[Tstate])
                    P.op("pool", lambda e, n=n, t0=t0, t1=t1: e.tensor_copy(hsum[:, t0:t1], Tb[:, 0:n]), reads=[TT], writes=[Thsum])
                else:
                    P.op("dve", lambda e, n=n, init=init: e.tensor_tensor_scan(rev(Tb[:, 0:n], n), rev(Rb[:, 0:n], n), rev(Ib[:, 0:n], n), init, ALU.mult, ALU.add),
                         reads=rds, writes=[TT])
                    P.op("dve", lambda e, n=n: e.tensor_copy(state[:, 0:1], Tb[:, 0:1]), reads=[TT], writes=[Tstate])
                    P.op("pool", lambda e, n=n, t0=t0, t1=t1: e.tensor_tensor(hsum[:, t0:t1], hsum[:, t0:t1], Tb[:, 0:n], ALU.add),
                         reads=[TT, Thsum], writes=[Thsum])
        P.dma("sp", G[2][:, :], pin_d[2], writes=[TG[2]])
        for (t0, t1, off) in SEGS_PAD:
            xin_ = G[2][:, t0 + off:t1 + off]
            P.op("pool", lambda e, t0=t0, t1=t1, xin_=xin_: e.tensor_tensor(G[3][:, t0:t1], xin_, xin_, ALU.mult), reads=[TG[2]], writes=[TG[3]])
            P.op("pool", lambda e, t0=t0, t1=t1: e.tensor_scalar(G[3][:, t0:t1], G[3][:, t0:t1], 0.044715, 1.0, ALU.mult, ALU.add), reads=[TG[3]], writes=[TG[3]])
            P.op("pool", lambda e, t0=t0, t1=t1, xin_=xin_: e.tensor_tensor(G[3][:, t0:t1], G[3][:, t0:t1], xin_, ALU.mult), reads=[TG[2], TG[3]], writes=[TG[3]])
            P.op("act", lambda e, t0=t0, t1=t1: e.activation(G[3][:, t0:t1], G[3][:, t0:t1], AF.Sigmoid, scale=1.5957691216057308), reads=[TG[3]], writes=[TG[3]])
            P.op("dve", lambda e, t0=t0, t1=t1, xin_=xin_: e.tensor_tensor(G[3][:, t0:t1], G[3][:, t0:t1], xin_, ALU.mult), reads=[TG[2], TG[3]], writes=[TG[3]])
        P.op("dve", lambda e: e.tensor_tensor(G[3][:, 0:NTOK], G[3][:, 0:NTOK], hsum[:, 0:NTOK], ALU.mult), reads=[TG[3], Thsum], writes=[TG[3]])
        P.dma("sp", lru_d, G[3][:, 0:NTOK], reads=[TG[3]])

        def hconv(src_i, grp, dst_i, eng):
            for (t0, t1, off) in SEGS_PAD:
                n = t1 - t0
                src = lambda k, t0=t0, off=off, n=n: G[src_i][:, t0 + off - 1 + k: t0 + off - 1 + k + n]
                dst = G[dst_i][:, t0:t1]
                P.op(eng, lambda e, src=src, dst=dst: e.tensor_scalar(
                    dst, src(0), hcw[:, grp, 0:1], hcb[:, grp:grp + 1], ALU.mult, ALU.add),
                    reads=[TG[src_i], Thcw, Thcb], writes=[TG[dst_i]])
                for k in range(1, 3):
                    P.op(eng, lambda e, src=src, dst=dst, k=k: e.scalar_tensor_tensor(
                        dst, src(k), hcw[:, grp, k:k + 1], dst, ALU.mult, ALU.add),
                        reads=[TG[src_i], Thcw], writes=[TG[dst_i]])

        P.dma("sp", G[0][:, :], pin_d[4], writes=[TG[0]])
        P.dma("sp", G[1][:, :], pin_d[5], writes=[TG[1]])
        hconv(0, 1, 2, "dve")
        hconv(1, 2, 3, "dve")
        P.op("dve", lambda e: e.tensor_tensor(G[2][:, 0:NTOK], G[2][:, 0:NTOK], G[3][:, 0:NTOK], ALU.mult), reads=[TG[2], TG[3]], writes=[TG[2]])
        P.dma("sp", scr_vx, G[2][:, 0:NTOK], reads=[TG[2]], writes=[])
        Tscr_vx = T("scr_vx"); Tscr_kf = T("scr_kf"); Tscr_kb = T("scr_kb"); Tscr_y = T("scr_y")
        P.dmas["sp"][-1].deps
        Tscr_vx.w = P.dmas["sp"][-1]
        P.dma("sp", G[0][:, :], pin_d[3], writes=[TG[0]])
        hconv(0, 0, 3, "dve")

        def mlp(z_ap, Tz, Lx, hA, ThA, hB, ThB):
            cur_in, Tcur, kdim, wmat_fn = z_ap, Tz, 33, (lambda: mw1[:, :])
            outs = [(hA, ThA), (hB, ThB), (hA, ThA)]
            for li in range(3):
                ob, Tob = outs[li]
                nb = (Lx + 511) // 512
                for b in range(nb):
                    c0 = b * 512; w = min(512, Lx - c0)
                    pi = nps()
                    if li == 0:
                        P.op("pe", lambda e, pi=pi, c0=c0, w=w, cur_in=cur_in: e.matmul(
                            pss[pi][0:64, 0:w], mw1[:, :], cur_in[0:33, c0:c0 + w], start=True, stop=True),
                            reads=[Tmw1, Tcur], writes=[Tps[pi]])
                    else:
                        P.op("pe", lambda e, pi=pi, c0=c0, w=w, cur_in=cur_in, li=li: e.matmul(
                            pss[pi][0:64, 0:w], mw23[:, li - 1, :], cur_in[0:64, c0:c0 + w], start=True, stop=True),
                            reads=[Tmw23, Tcur], writes=[Tps[pi]])
                    P.op("dve", lambda e, pi=pi, c0=c0, w=w, ob=ob, li=li: e.tensor_scalar(
                        ob[0:64, c0:c0 + w], pss[pi][0:64, 0:w], mb[:, li:li + 1], mb[:, 3:4], ALU.add, ALU.mult),
                        reads=[Tps[pi], Tmb], writes=[Tob])
                tb, Ttb = outs[(li + 1) % 2] if li < 2 else (hB, ThB)
                P.op("dve", lambda e, ob=ob, tb=tb: e.tensor_scalar(tb[0:64, 0:Lx], ob[0:64, 0:Lx], 1.0 / TWO_PI, MAGIC, ALU.mult, ALU.add),
                     reads=[Tob], writes=[Ttb])
                P.op("dve", lambda e, tb=tb: e.tensor_scalar(tb[0:64, 0:Lx], tb[0:64, 0:Lx], -MAGIC, -TWO_PI, ALU.add, ALU.mult),
                     reads=[Ttb], writes=[Ttb])
                P.op("dve", lambda e, ob=ob, tb=tb: e.tensor_tensor(ob[0:64, 0:Lx], ob[0:64, 0:Lx], tb[0:64, 0:Lx], ALU.add),
                     reads=[Tob, Ttb], writes=[Tob])
                P.op("act", lambda e, ob=ob: e.activation(ob[0:64, 0:Lx], ob[0:64, 0:Lx], AF.Sin), reads=[Tob], writes=[Tob])
                cur_in, Tcur = ob, Tob
            P.op("dve", lambda e: e.memset(hA[64:65, 0:Lx], 1.0), reads=[], writes=[ThA])
            return hA, ThA

        P.dma("sp", G[1][0:33, 0:L], zlat_d, writes=[TG[1]])
        h3, Th3 = mlp(G[1], TG[1], L, G[0], TG[0], G[1], TG[1])
        P.dma("sp", G[2][:, 0:L], declat_d, writes=[TG[2]])
        kfb = G[1]; Tkfb = TG[1]
        kbb = S[:, 0:2, :].rearrange("p a b -> p (a b)"); Tkbb = TS[0]
        for b in range(16):
            c0 = b * 512
            for fb, (dstb, Tdst) in enumerate([(kfb, Tkfb), (kbb, Tkbb)]):
                pi = nps()
                P.op("pe", lambda e, pi=pi, fb=fb, c0=c0: e.matmul(
                    pss[pi][:, :], w4[:, fb, :], h3[0:65, c0:c0 + 512], start=True, stop=True),
                    reads=[Tw4, Th3], writes=[Tps[pi]])
                P.op("dve", lambda e, pi=pi, dstb=dstb, c0=c0: e.tensor_tensor(
                    dstb[:, c0:c0 + 512], pss[pi][:, :], G[2][:, c0:c0 + 512], ALU.mult),
                    reads=[Tps[pi], TG[2]], writes=[Tdst])
        nrm = sb("nrm", [128, 4]); Tnrm = T("nrm")
        P.op("dve", lambda e: e.memset(kbb[:, 0:1], 0.0), reads=[Tkbb], writes=[Tkbb])
        P.op("dve", lambda e: e.tensor_reduce(nrm[:, 0:1], kfb[:, 0:L], AX.X, ALU.add, apply_absolute_value=True), reads=[Tkfb], writes=[Tnrm])
        P.op("dve", lambda e: e.tensor_reduce(nrm[:, 1:2], kbb[:, 0:L], AX.X, ALU.add, apply_absolute_value=True), reads=[Tkbb], writes=[Tnrm])
        P.op("dve", lambda e: e.tensor_tensor(nrm[:, 2:3], nrm[:, 0:1], nrm[:, 1:2], ALU.add), reads=[Tnrm], writes=[Tnrm])
        P.op("dve", lambda e: e.reciprocal(nrm[:, 3:4], nrm[:, 2:3]), reads=[Tnrm], writes=[Tnrm])
        o1 = P.dma("sp", scr_kf, kfb[:, 0:L], reads=[Tkfb]); Tscr_kf.w = o1
        o2 = P.dma("sp", scr_kb, kbb[:, 0:L], reads=[Tkbb]); Tscr_kb.w = o2

        XA = G[0][0:64, 0:3 * GC * 128].rearrange("p (s c n) -> p s c n", s=3, c=GC); TXA = TG[0]
        SA = S[:, 0:2, :].rearrange("p a b -> p (a b)")[:, 0:GC * 384].rearrange("p (c x) -> p c x", c=GC); TSA = TS[0]
        Kc = S[:, 2, :].rearrange("p (c x) -> p c x", c=GC); TKc = TS[2]
        Yb = G[2][:, 0:4096].rearrange("p (c x) -> p c x", c=GC); TYb = TG[2]
        BH = G[1][:, 0:4096].rearrange("p (c x) -> p c x", c=GC); TBH = TG[1]
        YG = G[1][0:64, 4096:4096 + GC * 128].rearrange("p (c n) -> p c n", c=GC); TYG = T("YG")
        tmps = [sb("tmp%d" % i, [128, 2, 128]) for i in range(8)]; Ttmp = [T("tmp%d" % i) for i in range(8)]
        tcnt = [0]
        srcs = [(scr_kf, Tscr_kf, 0), (scr_kb, Tscr_kb, 0), (scr_vx, Tscr_vx, 256)]

        def bc(ap2):
            return bass.AP(ap2.tensor, ap2.offset, [list(ap2.ap[0]), [0, 2], list(ap2.ap[1])])
        Trb = bc(Tw[:, 0, :]); Tib = bc(Tw[:, 1, :])

        def cmul(inA_r, inA_i, rd_in, tabr, tabi, rd_tab, out_r, out_i, out_nr, Tout, conj=False):
            par = (tcnt[0] % 2) * 4
            tcnt[0] += 1
            t1, t2, t3, t4 = [tmps[par + i] for i in range(4)]
            T1, T2, T3, T4 = [Ttmp[par + i] for i in range(4)]
            P.op("dve", lambda e: e.tensor_tensor(t1[:], inA_r, tabr, ALU.mult), reads=rd_in + rd_tab, writes=[T1])
            P.op("dve", lambda e: e.tensor_tensor(t2[:], inA_i, tabi, ALU.mult), reads=rd_in + rd_tab, writes=[T2])
            P.op("dve", lambda e: e.tensor_tensor(t3[:], inA_r, tabi, ALU.mult), reads=rd_in + rd_tab, writes=[T3])
            P.op("dve", lambda e: e.tensor_tensor(t4[:], inA_i, tabr, ALU.mult), reads=rd_in + rd_tab, writes=[T4])
            if not conj:
                P.op("pool", lambda e: e.tensor_tensor(out_r, t1[:], t2[:], ALU.subtract), reads=[T1, T2], writes=[Tout])
                if out_nr is not None:
                    P.op("pool", lambda e: e.tensor_tensor(out_nr, t2[:], t1[:], ALU.subtract), reads=[T1, T2], writes=[Tout])
                P.op("pool", lambda e: e.tensor_tensor(out_i, t3[:], t4[:], ALU.add), reads=[T3, T4], writes=[Tout])
            else:
                P.op("pool", lambda e: e.tensor_tensor(out_r, t1[:], t2[:], ALU.add), reads=[T1, T2], writes=[Tout])
                P.op("pool", lambda e: e.tensor_tensor(out_i, t4[:], t3[:], ALU.subtract), reads=[T3, T4], writes=[Tout])

        for g in range(128 // GC):
            ch0 = g * GC
            for s, (scr, Tscr, off) in enumerate(srcs):
                srcv = scr[ch0:ch0 + GC, off:off + L].rearrange("c (n1 n2) -> n1 c n2", n2=128)
                P.dma("sp", XA[:, s, :, :], srcv, reads=[Tscr], writes=[TXA])
            for s in range(3):
                for cp in range(GC // 2):
                    pi = nps()
                    for cc in range(2):
                        P.op("pe", lambda e, pi=pi, cc=cc, s=s, cp=cp: e.matmul(
                            pss[pi][:, cc * 256:(cc + 1) * 256], XA[:, s, cp * 2 + cc, :], F1[:, :], start=True, stop=True),
                            reads=[TXA, TF1], writes=[Tps[pi]])
                    pv = pss[pi][:, :].rearrange("p (c r k) -> p c r k", c=2, r=2)
                    cmul(pv[:, :, 0, :], pv[:, :, 1, :], [Tps[pi]], Trb, Tib, [TTw],
                         SA[:, cp * 2:cp * 2 + 2, 0:128], SA[:, cp * 2:cp * 2 + 2, 128:256], SA[:, cp * 2:cp * 2 + 2, 256:384], TSA)
                for cp in range(GC // 2):
                    pi = nps()
                    P.op("pe", lambda e, pi=pi, cp=cp: e.matmul(
                        pss[pi][:, :], F2[:, 0, :], SA[:, cp * 2:cp * 2 + 2, 0:256], start=True, stop=False),
                        reads=[TSA, TF2], writes=[Tps[pi]])
                    P.op("pe", lambda e, pi=pi, cp=cp: e.matmul(
                        pss[pi][:, :], F2[:, 1, :], SA[:, cp * 2:cp * 2 + 2, 128:384], start=False, stop=True),
                        reads=[TSA, TF2], writes=[Tps[pi]])
                    pv = pss[pi][:, :].rearrange("p (c r k) -> p c r k", c=2, r=2)
                    kv = Kc[:, cp * 2:cp * 2 + 2, :].rearrange("p c (r k) -> p c r k", r=2)
                    if s == 0:
                        P.op("act", lambda e, pv=pv, kv=kv: e.activation(kv, pv, AF.Copy), reads=[Tps[pi]], writes=[TKc])
                    elif s == 1:
                        P.op("dve", lambda e, pv=pv, kv=kv: e.tensor_tensor(kv[:, :, 0, :], kv[:, :, 0, :], pv[:, :, 0, :], ALU.add),
                             reads=[Tps[pi], TKc], writes=[TKc])
                        P.op("dve", lambda e, pv=pv, kv=kv: e.tensor_tensor(kv[:, :, 1, :], kv[:, :, 1, :], pv[:, :, 1, :], ALU.subtract),
                             reads=[Tps[pi], TKc], writes=[TKc])
                    else:
                        cmul(pv[:, :, 0, :], pv[:, :, 1, :], [Tps[pi]], kv[:, :, 0, :], kv[:, :, 1, :], [TKc],
                             Yb[:, cp * 2:cp * 2 + 2, 0:128], Yb[:, cp * 2:cp * 2 + 2, 128:256], None, TYb)
            for cp in range(GC // 2):
                pi = nps()
                for cc in range(2):
                    c = cp * 2 + cc
                    P.op("pe", lambda e, pi=pi, cc=cc, c=c: e.matmul(
                        pss[pi][:, cc * 256:(cc + 1) * 256], Yb[:, c, 0:128], GG[:, 0:256], start=True, stop=False),
                        reads=[TYb, TGG], writes=[Tps[pi]])
                    P.op("pe", lambda e, pi=pi, cc=cc, c=c: e.matmul(
                        pss[pi][:, cc * 256:(cc + 1) * 256], Yb[:, c, 128:256], GG[:, 256:512], start=False, stop=True),
                        reads=[TYb, TGG], writes=[Tps[pi]])
                pv = pss[pi][:, :].rearrange("p (c r k) -> p c r k", c=2, r=2)
                cmul(pv[:, :, 0, :], pv[:, :, 1, :], [Tps[pi]], Trb, Tib, [TTw],
                     BH[:, cp * 2:cp * 2 + 2, 0:128], BH[:, cp * 2:cp * 2 + 2, 128:256], None, TBH, conj=True)
            for c4 in range(GC // 4):
                pi = nps()
                P.op("pe", lambda e, pi=pi, c4=c4: e.matmul(
                    pss[pi][0:64, :], Cfin[:, 0, :], BH[:, c4 * 4:c4 * 4 + 4, 0:128], start=True, stop=False),
                    reads=[TBH, TCfin], writes=[Tps[pi]])
                P.op("pe", lambda e, pi=pi, c4=c4: e.matmul(
                    pss[pi][0:64, :], Cfin[:, 1, :], BH[:, c4 * 4:c4 * 4 + 4, 128:256], start=False, stop=True),
                    reads=[TBH, TCfin], writes=[Tps[pi]])
                P.op("act", lambda e, pi=pi, c4=c4: e.activation(
                    YG[:, c4 * 4:c4 * 4 + 4, :], pss[pi][0:64, :].rearrange("p (c n) -> p c n", c=4), AF.Copy),
                    reads=[Tps[pi]], writes=[TYG])
            dstv = scr_y[ch0:ch0 + GC, :].rearrange("c (n1 n2) -> n1 c n2", n2=128)
            oy = P.dma("sp", dstv, YG[:, :, :], reads=[TYG])
            Tscr_y.rd.append(oy)
        ydeps = list(Tscr_y.rd)

        zc = sb("zc", [65, LC]); Tzc = T("zc")
        hcA = sb("hcA", [65, LC]); ThcA = T("hcA")
        hcB = sb("hcB", [65, LC]); ThcB = T("hcB")
        P.dma("sp", zc[0:33, :], zctx_d, writes=[Tzc])
        h3c, Th3c = mlp(zc, Tzc, LC, hcA, ThcA, hcB, ThcB)
        kcf = sb("kcf", [128, 2, LC]); Tkcf = T("kcf")
        for fb in range(2):
            pi = nps()
            P.op("pe", lambda e, pi=pi, fb=fb: e.matmul(pss[pi][:, 0:LC], w4[:, fb, :], h3c[0:65, 0:LC], start=True, stop=True),
                 reads=[Tw4, Th3c], writes=[Tps[pi]])
            P.op("dve", lambda e, pi=pi, fb=fb: e.tensor_tensor(kcf[:, fb, :], pss[pi][:, 0:LC], decctx[:, :], ALU.mult),
                 reads=[Tps[pi], Tdecctx], writes=[Tkcf])
        nrc = sb("nrc", [128, 4]); Tnrc = T("nrc")
        P.op("dve", lambda e: e.memset(kcf[:, 1, 0:1], 0.0), reads=[Tkcf], writes=[Tkcf])
        P.op("dve", lambda e: e.tensor_reduce(nrc[:, 0:1], kcf[:, :, :].rearrange("p a b -> p (a b)"), AX.X, ALU.add, apply_absolute_value=True),
             reads=[Tkcf], writes=[Tnrc])
        P.op("dve", lambda e: e.reciprocal(nrc[:, 1:2], nrc[:, 0:1]), reads=[Tnrc], writes=[Tnrc])
        ov = P.dma("sp", G[0][:, 0:NTOK], scr_vx, reads=[Tscr_vx], writes=[TG[0]])
        oy2 = P.dma("sp", G[1][:, 256:NTOK], scr_y, reads=[], writes=[TG[1], TYG])
        oy2.deps.extend(ydeps)
        for dd in ydeps:
            dd.sig = True
        accD = G[2][:, 0:LC]; accP = G[2][:, LC:2 * LC]
        TaccD = T("accD"); TaccP = T("accP")
        vxc = G[0][:, 0:LC]
        P.op("dve", lambda e: e.memset(G[2][:, 0:2 * LC], 0.0), reads=[TG[2]], writes=[TG[2], TaccD, TaccP])
        lag_i = 0
        for tau in range(-(LC - 1), LC):
            ta = max(0, tau); tb_ = min(LC, LC + tau)
            kap = kcf[:, 0, tau:tau + 1] if tau >= 0 else kcf[:, 1, -tau:-tau + 1]
            eng, acc, Tacc = ("dve", accD, TaccD)
            lag_i += 1
            P.op(eng, lambda e, acc=acc, ta=ta, tb_=tb_, tau=tau, kap=kap: e.scalar_tensor_tensor(
                acc[:, ta:tb_], vxc[:, ta - tau:tb_ - tau], kap, acc[:, ta:tb_], ALU.mult, ALU.add),
                reads=[TG[0], Tkcf, Tacc], writes=[Tacc])
        P.op("dve", lambda e: e.tensor_tensor(G[1][:, 0:LC], accD, accP, ALU.add), reads=[TaccD, TaccP, TG[1]], writes=[TG[1]])
        for (t0, t1, rn) in [(0, 256, nrc[:, 1:2]), (256, NTOK, nrm[:, 3:4])]:
            P.op("dve", lambda e, t0=t0, t1=t1, rn=rn: e.tensor_scalar(G[1][:, t0:t1], G[1][:, t0:t1], rn, None, ALU.mult),
                 reads=[TG[1], Tnrm, Tnrc], writes=[TG[1]])
        P.op("dve", lambda e: e.scalar_tensor_tensor(G[1][:, 0:NTOK], G[0][:, 0:NTOK], hbias[:, 0:1], G[1][:, 0:NTOK], ALU.mult, ALU.add),
             reads=[TG[0], TG[1], Thbias], writes=[TG[1]])
        P.op("pool", lambda e: e.tensor_tensor(G[1][:, 0:NTOK], G[1][:, 0:NTOK], G[3][:, 0:NTOK], ALU.mult),
             reads=[TG[1], TG[3]], writes=[TG[1]])
        P.dma("sp", hy_d, G[1][:, 0:NTOK], reads=[TG[1]])
        P.finish()
    return nc

def mixer_inputs(d, l, j, pl, pc):
    head = j // 2; half = j % 2
    def chan(cols):
        a = np.zeros((len(cols), WP), np.float32)
        a[:, 1:257] = pc[:, cols].T
        a[:, 260:8452] = pl[:, cols].T
        return a
    r128 = np.arange(128)
    pin = np.stack([chan(head * 256 + r128), chan(head * 256 + 128 + r128), chan(1024 + 128 * j + r128),
                    chan(2048 + 128 * j + r128), chan(3072 + 128 * j + r128), chan(4096 + 128 * j + r128)], 0)
    m = dict(pin=pin)
    hc = head * 256 + np.arange(256)
    m['lcw'] = np.ascontiguousarray(d['lru_conv_w'][l][:, hc].reshape(4, 2, 128).transpose(2, 1, 0))
    m['lcb'] = np.ascontiguousarray(d['lru_conv_b'][l][hc].reshape(2, 128).T)
    oc = half * 128 + r128
    for nm, key in [('wa', 'lru_wa'), ('wx', 'lru_wx')]:
        w = d[key][l][:, head][:, :, oc]
        m[nm] = np.ascontiguousarray(w.reshape(2, 2, 128, 128).transpose(2, 0, 1, 3))
    ch = 128 * j + r128
    m['lvec'] = np.ascontiguousarray(np.stack([d['lru_ba'][l][:, ch], d['lru_bx'][l][:, ch], d['lru_lam'][l][:, ch]], -1).transpose(1, 0, 2))
    own = np.zeros((128, 2), np.float32); own[:, half] = 1.0
    m['own'] = own
    hcw = d['hy_conv_w'][l]; hcb = d['hy_conv_b'][l]
    m['hcw'] = np.ascontiguousarray(np.stack([hcw[:, g * 1024 + ch] for g in range(3)], 0).transpose(2, 0, 1))
    m['hcb'] = np.ascontiguousarray(np.stack([hcb[g * 1024 + ch] for g in range(3)], -1))
    m['hbias'] = np.ascontiguousarray(d['hy_bias'][l][ch][:, None])
    m['mw1'] = np.ascontiguousarray(d['filt_w1'][l])
    m['mw23'] = np.ascontiguousarray(np.stack([d['filt_w2'][l], d['filt_w3'][l]], 1))
    m['mb'] = np.ascontiguousarray(np.stack([d['filt_b1'][l], d['filt_b2'][l], d['filt_b3'][l], d['filt_freq'][l]], -1))
    w4 = d['filt_w4'][l]; b4 = d['filt_b4'][l]
    w4f = np.concatenate([w4[:, ch], b4[None, ch]], 0); w4b = np.concatenate([w4[:, 1024 + ch], b4[None, 1024 + ch]], 0)
    m['w4'] = np.ascontiguousarray(np.stack([w4f, w4b], 1))
    m.update(mconsts(j))
    return m


def _fm(v, n):
    return np.ascontiguousarray(np.asarray(v).reshape(n, 128).T)


_NC_CACHE = {}


def _get_nc(key):
    if key not in _NC_CACHE:
        if key == "M":
            _NC_CACHE[key] = build_M()
        else:
            _NC_CACHE[key] = build_T(*key)
    return _NC_CACHE[key]


def kernel(x, c, ctx, c_ctx, ada_w, ada_b, norm_g, ffn_wg, ffn_wu, ffn_wd, w_in, w_out, out_g,
           lru_conv_w, lru_conv_b, lru_wa, lru_ba, lru_wx, lru_bx, lru_lam,
           hy_conv_w, hy_conv_b, hy_bias, filt_w1, filt_b1, filt_w2, filt_b2, filt_w3, filt_b3,
           filt_w4, filt_b4, filt_freq, final_g):
    d = dict(lru_conv_w=lru_conv_w, lru_conv_b=lru_conv_b, lru_wa=lru_wa, lru_ba=lru_ba, lru_wx=lru_wx, lru_bx=lru_bx,
             lru_lam=lru_lam, hy_conv_w=hy_conv_w, hy_conv_b=hy_conv_b, hy_bias=hy_bias, filt_w1=filt_w1, filt_b1=filt_b1,
             filt_w2=filt_w2, filt_b2=filt_b2, filt_w3=filt_w3, filt_b3=filt_b3, filt_w4=filt_w4, filt_b4=filt_b4,
             filt_freq=filt_freq)
    d = {k: np.asarray(v, np.float32) for k, v in d.items()}
    f32 = np.float32
    x = np.asarray(x, f32); ctx = np.asarray(ctx, f32)
    NCORE = 8
    depth = ada_w.shape[0]
    cvec = np.ascontiguousarray(np.stack([np.asarray(c, f32)[0], np.asarray(c_ctx, f32)], -1).reshape(16, 128, 2).transpose(1, 0, 2))
    xs = [np.ascontiguousarray(np.concatenate([x[0, 1024 * i:1024 * (i + 1)], ctx[0, 32 * i:32 * (i + 1)]], 0).T) for i in range(NCORE)]
    mixs = None
    cores = list(range(NCORE))

    def t1_inputs(l):
        return dict(ada_w1=np.ascontiguousarray(ada_w[l][:, :5 * 2048]), ada_b1=_fm(ada_b[l][:5 * 2048], 80),
                    ng0=_fm(norm_g[l, 0], 16), ng1=_fm(norm_g[l, 1], 16),
                    wg1=np.asarray(ffn_wg[l, 0]), wu1=np.asarray(ffn_wu[l, 0]), wd1=np.asarray(ffn_wd[l, 0]),
                    w_in=np.asarray(w_in[l]))

    def t2_inputs(l):
        return dict(ada_w2=np.ascontiguousarray(ada_w[l][:, 5 * 2048:]), ada_b2=_fm(ada_b[l][5 * 2048:], 64),
                    og=_fm(out_g[l], 16), w_out=np.asarray(w_out[l]), ng2=_fm(norm_g[l, 2], 16),
                    wg2=np.asarray(ffn_wg[l, 1]), wu2=np.asarray(ffn_wu[l, 1]), wd2=np.asarray(ffn_wd[l, 1]))

    out = None
    for l in range(depth + 1):
        t2 = l > 0
        t1 = l < depth
        final = l == depth
        common = dict(cvec=cvec)
        if t2:
            common.update(t2_inputs(l - 1))
        if t1:
            common.update(t1_inputs(l))
        if final:
            common["fg"] = _fm(final_g, 16)
        in_maps = []
        for i in cores:
            m = dict(common); m["xT"] = xs[i]
            if t2:
                m["mix"] = mixs[i]
            in_maps.append(m)
        res = run_bass_kernel_spmd(_get_nc((t2, t1, final)), in_maps, core_ids=cores)
        if final:
            out = np.concatenate([res.results[i]["out"].T for i in cores], 0)[None]
            break
        xs = [res.results[i]["xo"] for i in cores]
        pls = [res.results[i]["pl"].T for i in cores]
        pl_full = np.concatenate([p[:1024] for p in pls], 0)
        pc_full = np.concatenate([p[1024:] for p in pls], 0)
        col_major = (l % 2 == 1)
        if col_major:
            pl_full = pl_full.reshape(128, 64, -1).transpose(1, 0, 2).reshape(8192, -1)
        m_maps = [mixer_inputs(d, l, j, pl_full, pc_full) for j in cores]
        resm = run_bass_kernel_spmd(_get_nc("M"), m_maps, core_ids=cores)
        mix_full = np.concatenate([resm.results[j]["lru"] for j in cores] + [resm.results[j]["hy"] for j in cores], 0)
        lat = mix_full[:, 256:]
        if col_major:
            lat = lat.reshape(2048, 64, 128).transpose(0, 2, 1).reshape(2048, 8192)
        cpart = mix_full[:, :256]
        mixs = [np.ascontiguousarray(np.concatenate([lat[:, 1024 * i:1024 * (i + 1)], cpart[:, 32 * i:32 * (i + 1)]], 1)) for i in cores]
    return np.ascontiguousarray(out.astype(np.float32))
```
